# Optimizing a Trainium2 kernel written in Bass

```python
import math
import jax, jax.numpy as jnp
from jax import lax
import numpy as np

D_MODEL = 1024
BATCH = 4
SEQ = 4096
DEPTH = 2

N_META = 16
BLOCK = 128
A_HEADS = 8
A_HEAD_DIM = 64
IDX_HEADS = 8
IDX_DIM = 64
TOPK_MAX = 256
N_BUCKETS = 32
MAX_DISTANCE = 128
B_HEADS = 4
B_QK_DIM = 128
B_V_DIM = 256
ROPE_BASE = 10000.0
C_HEADS = 16
C_HEAD_DIM = 64
D_FF = 2816
CONV_WIDTH = 3
EPS = 1e-6

A_Q = A_HEADS * A_HEAD_DIM
A_IQ = IDX_HEADS * IDX_DIM
B_QK = B_HEADS * B_QK_DIM
B_V = B_HEADS * B_V_DIM
EVEN_SPLITS = [A_Q, A_Q, A_Q, A_IQ, IDX_DIM, IDX_HEADS, B_QK, B_QK, B_V, B_V]
EVEN_IN = sum(EVEN_SPLITS)
EVEN_OUT = A_Q + B_V
C_W = C_HEADS * C_HEAD_DIM
ODD_IN = 3 * C_W

kernel_name = "hybrid_dsa_retention_stickbreaking_convffn"


def rmsnorm(x, g):
    xf = x.astype(jnp.float32)
    y = xf * lax.rsqrt(jnp.mean(jnp.square(xf), axis=-1, keepdims=True) + EPS)
    return (y * g.astype(jnp.float32)).astype(x.dtype)


def t5_bucket(rel):
    n = jnp.maximum(rel, 0)
    max_exact = N_BUCKETS // 2
    large = max_exact + (jnp.log(jnp.maximum(n, 1).astype(jnp.float32) / max_exact)
                         / math.log(MAX_DISTANCE / max_exact)
                         * (N_BUCKETS - max_exact)).astype(jnp.int32)
    large = jnp.minimum(large, N_BUCKETS - 1)
    return jnp.where(n < max_exact, n, large)


def rotary(x, pos):
    half = x.shape[-1] // 2
    inv = 1.0 / (ROPE_BASE ** (jnp.arange(half, dtype=jnp.float32) / half))
    ang = pos.astype(jnp.float32)[:, None] * inv[None, :]
    cos = jnp.cos(ang)[:, None, :]
    sin = jnp.sin(ang)[:, None, :]
    xf = x.astype(jnp.float32)
    x1, x2 = xf[..., :half], xf[..., half:]
    return jnp.concatenate([x1 * cos - x2 * sin, x2 * cos + x1 * sin], axis=-1).astype(x.dtype)


def dsa_attention(q, k, v, qi, ki, wi, rel_bias, n_keep):
    Bsz, L = q.shape[:2]
    nblk = L // BLOCK
    kpos = jnp.arange(L)
    is_meta = kpos < N_META

    def block(args):
        qb, qib, wib, qpos = args
        s = jnp.einsum('bqhd,bkd->bqhk', qib, ki).astype(jnp.float32) * (IDX_DIM ** -0.5)
        score = jnp.einsum('bqhk,bqh->bqk', jax.nn.relu(s), wib.astype(jnp.float32))
        causal = kpos[None, :] <= qpos[:, None]
        score = jnp.where(causal[None], jnp.where(is_meta[None, None, :], jnp.inf, score), -jnp.inf)
        _, idx = lax.top_k(score, n_keep)
        gather = jax.vmap(lambda a, i: a[i])
        ksel = gather(k, idx)
        vsel = gather(v, idx)
        rel = qpos[None, :, None] - idx
        bias = jnp.moveaxis(rel_bias[t5_bucket(rel)], -1, 2)
        logits = jnp.einsum('bqhd,bqkhd->bqhk', qb, ksel).astype(jnp.float32) * (A_HEAD_DIM ** -0.5)
        logits = logits + bias.astype(jnp.float32)
        logits = jnp.where((rel >= 0)[:, :, None, :], logits, -jnp.inf)
        p = jax.nn.softmax(logits, axis=-1).astype(v.dtype)
        return jnp.einsum('bqhk,bqkhd->bqhd', p, vsel)

    to_blocks = lambda a: jnp.moveaxis(a.reshape(Bsz, nblk, BLOCK, *a.shape[2:]), 1, 0)
    out = lax.map(block, (to_blocks(q), to_blocks(qi), to_blocks(wi), kpos.reshape(nblk, BLOCK)))
    return jnp.moveaxis(out, 0, 1).reshape(Bsz, L, *q.shape[2:])


def retention(q, k, v):
    Bsz, L, H, dk = q.shape
    dv = v.shape[-1]
    C = BLOCK
    N = L // C
    lg = jnp.log(1.0 - 2.0 ** (-5.0 - jnp.arange(H, dtype=jnp.float32)))
    n = jnp.arange(C, dtype=jnp.float32)
    diff = n[:, None] - n[None, :]
    dmat = jnp.where(diff[None] >= 0, jnp.exp(jnp.maximum(diff, 0.0)[None] * lg[:, None, None]), 0.0)
    xi = jnp.exp((n[None, :] + 1.0) * lg[:, None])[..., None]
    zeta = jnp.exp((C - 1.0 - n[None, :]) * lg[:, None])[..., None]
    g_chunk = jnp.exp(C * lg)[:, None, None]
    chunks = lambda a: a.astype(jnp.float32).reshape(Bsz, N, C, H, a.shape[-1]).transpose(1, 0, 3, 2, 4)
    qc, kc, vc = chunks(q), chunks(k * (dk ** -0.5)), chunks(v)

    def step(R, inp):
        qi, ki, vi = inp
        inner = jnp.einsum('bhcd,bhed->bhce', qi, ki) * dmat
        o = jnp.einsum('bhce,bhef->bhcf', inner, vi) + jnp.einsum('bhcd,bhdf->bhcf', qi, R) * xi
        R = R * g_chunk + jnp.einsum('bhcd,bhcf->bhdf', ki * zeta, vi)
        return R, o

    R0 = jnp.zeros((Bsz, H, dk, dv), jnp.float32)
    _, o = lax.scan(step, R0, (qc, kc, vc))
    return o.transpose(1, 0, 3, 2, 4).reshape(Bsz, L, H, dv)


def stick_breaking(q, k, v):
    L = q.shape[1]
    scale = q.shape[-1] ** -0.5
    outs = []
    for i in range(L // BLOCK):
        s, e = i * BLOCK, (i + 1) * BLOCK
        z = jnp.einsum('bqhd,bkhd->bhqk', q[:, s:e], k[:, :e]).astype(jnp.float32) * scale
        mask = (jnp.arange(e)[None, :] < jnp.arange(s, e)[:, None])[None, None]
        log_1m = jnp.where(mask, jax.nn.log_sigmoid(-z), 0.0)
        rest = lax.cumsum(log_1m, axis=3, reverse=True) - log_1m
        a = jnp.where(mask, jnp.exp(jax.nn.log_sigmoid(z) + rest), 0.0)
        outs.append(jnp.einsum('bhqk,bkhd->bqhd', a.astype(v.dtype), v[:, :e]))
    return jnp.concatenate(outs, axis=1)


def even_mixer(h, w_in, gn_gain, w_out, rel_bias, n_keep):
    Bsz, L, _ = h.shape
    p = h @ w_in
    splits = [int(s) for s in np.cumsum(EVEN_SPLITS)[:-1]]
    aq, ak, av, iq, ik, iw, bq, bk, bv, bg = jnp.split(p, splits, axis=-1)
    heads = lambda t, nh: t.reshape(Bsz, L, nh, -1)
    a_out = dsa_attention(heads(aq, A_HEADS), heads(ak, A_HEADS), heads(av, A_HEADS),
                          heads(iq, IDX_HEADS), ik, iw * (IDX_HEADS ** -0.5), rel_bias, n_keep)
    a_out = a_out.reshape(Bsz, L, A_Q)
    pos = jnp.arange(L)
    bq = rotary(heads(bq, B_HEADS), pos)
    bk = rotary(heads(bk, B_HEADS), pos)
    lead = (-N_META) % BLOCK
    frame = lambda t: jnp.pad(t[:, :L - lead], ((0, 0), (lead, 0), (0, 0), (0, 0)))
    r = retention(frame(bq), frame(bk), frame(heads(bv, B_HEADS)))
    r = jnp.pad(r[:, lead:], ((0, 0), (0, lead), (0, 0), (0, 0)))
    mu = jnp.mean(r, axis=-1, keepdims=True)
    var = jnp.mean(jnp.square(r - mu), axis=-1, keepdims=True)
    r = ((r - mu) * lax.rsqrt(var + EPS)).reshape(Bsz, L, B_V) * gn_gain.astype(jnp.float32)
    r_out = (r * jax.nn.silu(bg.astype(jnp.float32))).astype(h.dtype)
    return jnp.concatenate([a_out, r_out], axis=-1) @ w_out


def odd_mixer(h, w_in, w_out):
    Bsz, L, _ = h.shape
    q, k, v = jnp.split(h @ w_in, 3, axis=-1)
    heads = lambda t: t.reshape(Bsz, L, C_HEADS, C_HEAD_DIM)
    return stick_breaking(heads(q), heads(k), heads(v)).reshape(Bsz, L, C_W) @ w_out


def conv_ffn(h, w_up, w_gate, conv_w, conv_b, w_down):
    u = h @ w_up
    g = h @ w_gate
    g = lax.conv_general_dilated(g, conv_w[:, None, :].astype(g.dtype), window_strides=(1,),
                                 padding=[(CONV_WIDTH - 1, 0)],
                                 dimension_numbers=('NWC', 'WIO', 'NWC'),
                                 feature_group_count=D_FF) + conv_b
    return (jax.nn.silu(g) * u) @ w_down


def setup_inputs(seed: int = 0) -> dict:
    key = jax.random.key(seed)
    ks = jax.random.split(key, 20)
    f32 = jnp.float32
    ne, no = (DEPTH + 1) // 2, DEPTH // 2
    nrm = lambda k, shape, fan_in: jax.random.normal(k, shape, f32) * (fan_in ** -0.5)
    gain = lambda k, shape: 1.0 + 0.05 * jax.random.normal(k, shape, f32)
    return {
        "x": jax.random.normal(ks[0], (BATCH, SEQ, D_MODEL), f32),
        "meta_tokens": jax.random.normal(ks[1], (N_META, D_MODEL), f32),
        "rel_bias": 0.5 * jax.random.normal(ks[2], (N_BUCKETS, A_HEADS), f32),
        "norm_mix": gain(ks[3], (DEPTH, D_MODEL)),
        "norm_ffn": gain(ks[4], (DEPTH, D_MODEL)),
        "norm_final": gain(ks[5], (D_MODEL,)),
        "even_w_in": nrm(ks[6], (ne, D_MODEL, EVEN_IN), D_MODEL),
        "even_gn_gain": gain(ks[7], (ne, B_V)),
        "even_w_out": nrm(ks[8], (ne, EVEN_OUT, D_MODEL), EVEN_OUT),
        "odd_w_in": nrm(ks[9], (no, D_MODEL, ODD_IN), D_MODEL),
        "odd_w_out": nrm(ks[10], (no, C_W, D_MODEL), C_W),
        "ffn_w_up": nrm(ks[11], (DEPTH, D_MODEL, D_FF), D_MODEL),
        "ffn_w_gate": nrm(ks[12], (DEPTH, D_MODEL, D_FF), D_MODEL),
        "ffn_conv_w": nrm(ks[13], (DEPTH, CONV_WIDTH, D_FF), CONV_WIDTH),
        "ffn_conv_b": 0.01 * jax.random.normal(ks[14], (DEPTH, D_FF), f32),
        "ffn_w_down": nrm(ks[15], (DEPTH, D_FF, D_MODEL), D_FF),
    }


def reference(x, meta_tokens, rel_bias, norm_mix, norm_ffn, norm_final,
              even_w_in, even_gn_gain, even_w_out, odd_w_in, odd_w_out,
              ffn_w_up, ffn_w_gate, ffn_conv_w, ffn_conv_b, ffn_w_down):
    Bsz, S, D = x.shape
    lead = (-N_META) % BLOCK
    n_keep = min(TOPK_MAX, S // 4)
    meta = jnp.broadcast_to(meta_tokens[None].astype(x.dtype), (Bsz, N_META, D))
    h = jnp.concatenate([meta, x, jnp.zeros((Bsz, lead, D), x.dtype)], axis=1)
    for l in range(DEPTH):
        hn = rmsnorm(h, norm_mix[l])
        if l % 2 == 0:
            j = l // 2
            h = h + even_mixer(hn, even_w_in[j], even_gn_gain[j], even_w_out[j], rel_bias, n_keep)
        else:
            j = l // 2
            h = h + odd_mixer(hn, odd_w_in[j], odd_w_out[j])
        h = h + conv_ffn(rmsnorm(h, norm_ffn[l]), ffn_w_up[l], ffn_w_gate[l],
                         ffn_conv_w[l], ffn_conv_b[l], ffn_w_down[l])
    return rmsnorm(h, norm_final)[:, N_META:N_META + S]
```

```python
from contextlib import ExitStack
import math
import numpy as np
import ml_dtypes
import concourse.bass as bass
import concourse.mybir as mybir
from concourse.bass_utils import run_bass_kernel_spmd

F32 = mybir.dt.float32
BF16 = mybir.dt.bfloat16
AF = mybir.ActivationFunctionType
ALU = mybir.AluOpType
AX = mybir.AxisListType
NPBF = ml_dtypes.bfloat16

COMPUTE = ("pe", "act", "dve", "pool")
QUEUES = ("sp", "qact", "qpool")
DMA_K = 8
EPOCH = 24000


class Res:
    __slots__ = ("name", "last_w", "readers", "excl")

    def __init__(self, name):
        self.name = name
        self.last_w = None
        self.readers = []
        self.excl = False


class Ins:
    __slots__ = ("eng", "fn", "deps", "signal", "sem", "val", "is_dma", "q", "done")

    def __init__(self, eng, fn, is_dma=False, q=None):
        self.eng = eng
        self.fn = fn
        self.deps = []
        self.signal = False
        self.sem = None
        self.val = 0
        self.is_dma = is_dma
        self.q = q
        self.done = False


class Prog:
    def __init__(self, nc, stack):
        self.nc = nc
        self.stack = stack
        self.streams = {"pe": nc.tensor, "act": nc.scalar, "dve": nc.vector,
                        "pool": nc.gpsimd, "sp": nc.sync}
        self.qstream = {"sp": "sp", "qact": "act", "qpool": "pool"}
        self.pending = []
        self.sem_cnt = {e: 0 for e in COMPUTE}
        self.sems = {e: [] for e in COMPUTE}
        self.dsems = {q: [stack.enter_context(nc.semaphore(f"d_{q}_{k}")) for k in range(DMA_K)]
                      for q in QUEUES}
        self.dcnt = {q: 0 for q in QUEUES}
        self.dhist = {q: [] for q in QUEUES}
        self.seen = {s: {} for s in self.streams}
        self.last = {e: None for e in COMPUTE}
        self.nres = 0
        self.n_ins = 0
        self.n_wait = 0

    def res(self, name=None):
        self.nres += 1
        return Res(name or f"r{self.nres}")

    def _sem_for(self, eng, n):
        k = n // EPOCH
        while len(self.sems[eng]) <= k:
            self.sems[eng].append(self.stack.enter_context(
                self.nc.semaphore(f"s_{eng}_{len(self.sems[eng])}")))
        return self.sems[eng][k]

    def _track(self, ins, reads, writes):
        deps = []
        for r in reads:
            if r.last_w is not None:
                deps.append(r.last_w)
        for w in writes:
            if w.last_w is not None:
                deps.append(w.last_w)
            deps.extend(w.readers)
        uniq = []
        sid = set()
        for d in deps:
            if id(d) in sid or d is ins or d.done:
                continue
            if (not d.is_dma) and (not ins.is_dma) and d.eng == "pe" and ins.eng == "pe":
                continue
            sid.add(id(d))
            uniq.append(d)
            d.signal = True
        ins.deps = uniq
        for r in reads:
            if ins.is_dma:
                r.readers.append(ins)
            else:
                r.readers = [x for x in r.readers if x.is_dma or x.eng != ins.eng]
                r.readers.append(ins)
        for w in writes:
            w.last_w = ins
            w.readers = []

    def op(self, eng, fn, reads=(), writes=()):
        ins = Ins(eng, fn)
        self._track(ins, list(reads), list(writes))
        self.pending.append(ins)
        self.last[eng] = ins
        return ins

    def dma(self, q, out, in_, reads=(), writes=(), **kw):
        def fn(e, out=out, in_=in_, kw=kw):
            return e.dma_start(out=out, in_=in_, **kw)
        ins = Ins(self.qstream[q], fn, is_dma=True, q=q)
        self._track(ins, list(reads), list(writes))
        ins.signal = True
        hist = self.dhist[q]
        if len(hist) >= DMA_K:
            prev = hist[-DMA_K]
            if (not prev.done) and prev not in ins.deps:
                ins.deps.append(prev)
        hist.append(ins)
        if len(hist) > 4 * DMA_K:
            del hist[:-2 * DMA_K]
        self.pending.append(ins)
        return ins

    def barrier_deps(self):
        deps = []
        for e in COMPUTE:
            if self.last[e] is not None:
                deps.append(self.last[e])
        for q in QUEUES:
            deps.extend(self.dhist[q][-DMA_K:])
        return deps

    def flush(self, final=False):
        nc = self.nc
        bdeps = self.barrier_deps()
        for d in bdeps:
            d.signal = True
        for ins in self.pending:
            if ins.sem is not None or not ins.signal:
                continue
            if ins.is_dma:
                n = self.dcnt[ins.q]
                self.dcnt[ins.q] = n + 1
                ins.sem = self.dsems[ins.q][n % DMA_K]
                ins.val = 16 * (n // DMA_K + 1)
            else:
                n = self.sem_cnt[ins.eng]
                self.sem_cnt[ins.eng] = n + 1
                ins.sem = self._sem_for(ins.eng, n)
                ins.val = n % EPOCH + 1
        per = {s: [] for s in self.streams}
        for ins in self.pending:
            per[ins.eng].append(ins)
        prog = self

        def run_stream(sname, eng):
            seen = prog.seen[sname]
            for ins in per[sname]:
                for d in ins.deps:
                    key = id(d.sem)
                    if seen.get(key, 0) < d.val:
                        eng.wait_ge(d.sem, d.val)
                        seen[key] = d.val
                        prog.n_wait += 1
                bi = ins.fn(eng)
                prog.n_ins += 1
                if ins.signal:
                    bi.then_inc(ins.sem, 16 if ins.is_dma else 1)
            for d in bdeps:
                key = id(d.sem)
                if seen.get(key, 0) < d.val:
                    eng.wait_ge(d.sem, d.val)
                    seen[key] = d.val

        with nc.Block() as block:
            @block.tensor
            def _(e):
                run_stream("pe", e)

            @block.scalar
            def _(e):
                run_stream("act", e)

            @block.vector
            def _(e):
                run_stream("dve", e)

            @block.gpsimd
            def _(e):
                run_stream("pool", e)

            @block.sync
            def _(e):
                run_stream("sp", e)
        for ins in self.pending:
            ins.done = True
            ins.fn = None
            ins.deps = []
        self.pending = []


class V:
    __slots__ = ("ap", "res")

    def __init__(self, ap, res):
        self.ap = ap
        self.res = res


class _Part:
    def __init__(self, t, res):
        self.t = t
        self.res = res

    def __getitem__(self, idx):
        return V(self.t[idx], self.res)


class T:
    def __init__(self, prog, t, name, track=True):
        self.t = t
        self.prog = prog
        self.name = name
        self.r = prog.res(name)
        self.sub = {}
        self.track = track

    def __getitem__(self, idx):
        return V(self.t[idx], [self.r] if self.track else [])

    def p(self, key):
        if key not in self.sub:
            self.sub[key] = self.prog.res(f"{self.name}.{key}")
        return _Part(self.t, [self.sub[key]])

    def all(self, idx=slice(None)):
        return V(self.t[idx], [self.r] + list(self.sub.values()))


class KB:
    def __init__(self, nc, stack):
        self.nc = nc
        self.P = Prog(nc, stack)
        self.alt = 0

    def sb(self, st, name, shape, dt):
        self.uid = getattr(self, "uid", 0) + 1
        name = f"s{self.uid}_{name}"
        return T(self.P, st.enter_context(self.nc.sbuf_tensor(name, list(shape), dt)), name)

    def ps(self, st, name, shape, dt=F32):
        self.uid = getattr(self, "uid", 0) + 1
        name = f"p{self.uid}_{name}"
        t = T(self.P, st.enter_context(self.nc.psum_tensor(name, list(shape), dt)), name)
        t.r.excl = True
        return t

    def dram(self, name, shape, dt, kind="Internal"):
        t = self.nc.dram_tensor(name, list(shape), dt, kind=kind)
        return T(self.P, t.ap(), name, track=False)

    def _rw(self, outs, ins):
        w = []
        for o in outs:
            w.extend(o.res)
        r = []
        for i in ins:
            if isinstance(i, V):
                r.extend(i.res)
                for x in i.res:
                    if x.excl:
                        w.append(x)
        return r, w

    def dma(self, q, out, in_, **kw):
        r, w = self._rw([out], [in_])
        return self.P.dma(q, out.ap, in_.ap, reads=r, writes=w, **kw)

    def mm(self, out, lhsT, rhs, start=True, stop=True):
        r, w = self._rw([out], [lhsT, rhs])
        if not start:
            r = r + w
        self.P.op("pe", lambda e: e.matmul(out.ap, lhsT=lhsT.ap, rhs=rhs.ap, start=start, stop=stop,
                                           skip_group_check=True), reads=r, writes=w)

    def tr(self, out, in_, ident):
        r, w = self._rw([out], [in_, ident])
        self.P.op("pe", lambda e: e.transpose(out=out.ap, in_=in_.ap, identity=ident.ap), reads=r, writes=w)

    def act(self, out, in_, func, bias=None, scale=None, accum=None, eng="act"):
        outs = [out] + ([accum] if accum is not None else [])
        ins = [in_] + [x for x in (bias, scale) if isinstance(x, V)]
        r, w = self._rw(outs, ins)
        kw = {}
        if bias is not None:
            kw["bias"] = bias.ap if isinstance(bias, V) else bias
        if scale is not None:
            kw["scale"] = scale.ap if isinstance(scale, V) else scale
        if accum is not None:
            kw["accum_out"] = accum.ap
        self.P.op("act", lambda e: e.activation(out=out.ap, in_=in_.ap, func=func, **kw), reads=r, writes=w)

    def ts(self, eng, out, in0, s1, s2, op0, op1=None, accum=None):
        outs = [out] + ([accum] if accum is not None else [])
        ins = [in0] + [x for x in (s1, s2) if isinstance(x, V)]
        r, w = self._rw(outs, ins)
        a1 = s1.ap if isinstance(s1, V) else s1
        a2 = s2.ap if isinstance(s2, V) else s2
        kw = {}
        if op1 is not None:
            kw["op1"] = op1
        if accum is not None:
            kw["accum_out"] = accum.ap
            r = r + list(accum.res)
        self.P.op(eng, lambda e: e.tensor_scalar(out=out.ap, in0=in0.ap, scalar1=a1, scalar2=a2, op0=op0, **kw),
                  reads=r, writes=w)

    def tt(self, eng, out, in0, in1, op):
        r, w = self._rw([out], [in0, in1])
        self.P.op(eng, lambda e: e.tensor_tensor(out=out.ap, in0=in0.ap, in1=in1.ap, op=op), reads=r, writes=w)

    def stt(self, eng, out, in0, scalar, in1, op0, op1):
        ins = [in0, in1] + ([scalar] if isinstance(scalar, V) else [])
        r, w = self._rw([out], ins)
        sc = scalar.ap if isinstance(scalar, V) else scalar
        self.P.op(eng, lambda e: e.scalar_tensor_tensor(out=out.ap, in0=in0.ap, scalar=sc, in1=in1.ap,
                                                        op0=op0, op1=op1), reads=r, writes=w)

    def copy(self, eng, out, in_):
        r, w = self._rw([out], [in_])
        if eng == "act":
            self.P.op("act", lambda e: e.copy(out=out.ap, in_=in_.ap), reads=r, writes=w)
        else:
            self.P.op(eng, lambda e: e.tensor_copy(out=out.ap, in_=in_.ap), reads=r, writes=w)

    def memset(self, eng, out, val):
        r, w = self._rw([out], [])
        self.P.op(eng, lambda e: e.memset(out.ap, val), reads=r, writes=w)

    def recip(self, out, in_):
        r, w = self._rw([out], [in_])
        self.P.op("dve", lambda e: e.reciprocal(out=out.ap, in_=in_.ap), reads=r, writes=w)

    def evac(self, out, in_):
        self.alt ^= 1
        self.copy("act" if self.alt else "dve", out, in_)


class Ring:
    def __init__(self, items):
        self.items = items
        self.i = 0

    def next(self):
        x = self.items[self.i % len(self.items)]
        self.i += 1
        return x


D = 1024
NS = 17
NT = NS * 128
NB = 34
NG = NB * 128
P0 = 240
GROUPS = [(0, 4), (4, 4), (8, 4), (12, 4), (16, 1)]
DFF = 2816
NFC = DFF // 128
C_AQ, C_AK, C_AV, C_IQ, C_IK, C_IW, C_BQ, C_BK, C_BV, C_BG = 0, 512, 1024, 1536, 2048, 2112, 2120, 2632, 3144, 4168
NEG = -30000.0
BIG = 1.0e30


def own_blocks(c):
    return [2 * i + 1 - c for i in range(NS)]


def gstream(xb, meta):
    g = np.zeros((NG, D), np.float32)
    g[P0:P0 + 16] = meta
    g[256:256 + 4096] = xb
    return g


def take_own(garr, c):
    return np.concatenate([garr[B * 128:(B + 1) * 128] for B in own_blocks(c)], axis=0)


def merge_pair(a0, a1, axis=0):
    a0 = np.moveaxis(a0, axis, 0)
    a1 = np.moveaxis(a1, axis, 0)
    out = np.empty((NG,) + a0.shape[1:], a0.dtype)
    for i in range(NS):
        out[(2 * i + 1) * 128:(2 * i + 2) * 128] = a0[i * 128:(i + 1) * 128]
        out[(2 * i) * 128:(2 * i + 1) * 128] = a1[i * 128:(i + 1) * 128]
    return np.moveaxis(out, 0, axis)


def rep128(v):
    return np.ascontiguousarray(np.broadcast_to(np.asarray(v, np.float32).reshape(1, -1), (128, v.size)))


def rot_tables(c):
    pos = np.concatenate([np.arange(B * 128, (B + 1) * 128) for B in own_blocks(c)]) - P0
    pos = np.maximum(pos, 0).astype(np.float32)
    inv = (1.0 / (10000.0 ** (np.arange(64, dtype=np.float32) / 64.0))).astype(np.float32)
    ang = (pos[:, None] * inv[None, :]).astype(np.float32)
    cs, sn = np.cos(ang.astype(np.float64)), np.sin(ang.astype(np.float64))
    ks = 128.0 ** -0.5
    return (cs.astype(np.float32), sn.astype(np.float32), (cs * ks).astype(np.float32), (sn * ks).astype(np.float32))


def emit_norm_T(kb, xt, g32, ident, hnT_dst, scr):
    kb.act(scr["junk"][:], xt, AF.Square, accum=scr["ssq"][:])
    kb.act(scr["rstd"][:], scr["ssq"][:], AF.Ln, bias=1024 * 1e-6)
    kb.act(scr["rstd"][:], scr["rstd"][:], AF.Exp, scale=-0.5)
    kb.stt("dve", scr["hn"][:], xt, scr["rstd"][:, 0:1], g32, ALU.mult, ALU.mult)
    pT = scr["pT"].next()
    for kc in range(8):
        kb.tr(pT[:, kc, :], scr["hn"][:, kc * 128:(kc + 1) * 128], ident)
    kb.evac(hnT_dst, pT[:])


def load_w_cast(kb, wt_part, w_dram, col0, ncols, dcol0=0, chunk=512, kparts=8):
    c = 0
    while c < ncols:
        n = min(chunk, ncols - c)
        src = w_dram.t[:, col0 + c:col0 + c + n].rearrange("(kc p) m -> p kc m", p=128)
        dst = wt_part[:, :, dcol0 + c:dcol0 + c + n]
        kb.dma("qpool", dst, V(src, []))
        c += n


def phase_proj0(kb, io):
    with ExitStack() as st:
        sb = lambda n, s, d: kb.sb(st, n, s, d)
        wt = sb("wt", [128, 8, 5192 + 64], BF16)
        ident = sb("ident", [128, 128], BF16)
        g32 = sb("g32", [128, 1024], F32)
        cosq = sb("cosq", [128, NS, 64], F32)
        sinq = sb("sinq", [128, NS, 64], F32)
        cosk = sb("cosk", [128, NS, 64], F32)
        sink = sb("sink", [128, NS, 64], F32)
        xts = Ring([sb(f"xt{i}", [128, 1024], F32) for i in range(2)])
        hnT = Ring([sb(f"hnT{i}", [128, 8, 512], BF16) for i in range(2)])
        scr = {"junk": sb("junk", [128, 1024], F32), "ssq": sb("ssq", [128, 1], F32),
               "rstd": sb("rstd", [128, 1], F32), "hn": sb("hn", [128, 1024], BF16),
               "pT": Ring([kb.ps(st, f"pT{i}", [128, 8, 128], BF16) for i in range(2)])}
        pF = Ring([kb.ps(st, f"pF{i}", [128, 512], F32) for i in range(2)])
        pM = Ring([kb.ps(st, f"pM{i}", [128, 512], F32) for i in range(3)])
        stF = Ring([sb(f"stF{i}", [128, 512], BF16) for i in range(3)])
        stM = Ring([sb(f"stM{i}", [128, 512], BF16) for i in range(3)])
        stG = Ring([sb(f"stG{i}", [128, 512], F32) for i in range(2)])
        stW = Ring([sb(f"stW{i}", [128, 8], F32) for i in range(2)])
        ra = Ring([sb(f"ra{i}", [128, 4, 64], F32) for i in range(2)])
        rb = Ring([sb(f"rb{i}", [128, 4, 64], F32) for i in range(2)])

        kb.dma("sp", ident[:], io["ident"][:])
        kb.dma("sp", g32[:], io["gmix0"][:])
        kb.ts("dve", g32[:], g32[:], 32.0, None, ALU.mult)
        for nm, tt_ in (("cosq", cosq), ("sinq", sinq), ("cosk", cosk), ("sink", sink)):
            kb.dma("sp", tt_[:], V(io[nm].t.rearrange("(s p) f -> p s f", p=128), []))
        w = io["w_in"]
        for c0 in range(0, 5192, 512):
            n = min(512, 5192 - c0)
            load_w_cast(kb, wt.p(c0 // 512), w, c0, n, dcol0=c0)
        load_w_cast(kb, wt.p("ikdup"), w, C_IK, 64, dcol0=5192)

        def wcols(c0, n):
            res = []
            for pc in range(c0 // 512, (c0 + n - 1) // 512 + 1):
                res.extend(wt.p(pc).res)
            return res

        fchunks = [("QT", C_AQ + 128 * j, j) for j in range(4)] + [("KT", C_AK + 128 * j, j) for j in range(4)] + \
                  [("QIT", C_IQ + 128 * j, j) for j in range(4)]
        for (s0, ns) in GROUPS:
            ntok = ns * 128
            hT = hnT.next()
            for sl in range(ns):
                s = s0 + sl
                xt = xts.next()
                kb.dma("sp", xt[:], io["xs"][s * 128:(s + 1) * 128, :])
                emit_norm_T(kb, xt[:], g32[:], ident[:], hT[:, :, sl * 128:(sl + 1) * 128], scr)
            for (nm, c0, j) in fchunks:
                ps = pF.next()
                res = wcols(c0, 128)
                for kc in range(8):
                    kb.mm(ps[:, 0:ntok], V(wt.t[:, kc, c0:c0 + 128], res), hT[:, kc, 0:ntok], start=(kc == 0), stop=(kc == 7))
                sg = stF.next()
                kb.evac(sg[:, 0:ntok], ps[:, 0:ntok])
                kb.dma("sp", io[nm][j * 128:(j + 1) * 128, s0 * 128:s0 * 128 + ntok], sg[:, 0:ntok])
            ps = pF.next()
            for kc in range(8):
                kb.mm(ps[0:64, 0:ntok], V(wt.t[:, kc, C_IK:C_IK + 64], wcols(C_IK, 64)), hT[:, kc, 0:ntok], start=(kc == 0), stop=(kc == 7))
            for kc in range(8):
                kb.mm(ps[64:128, 0:ntok], V(wt.t[:, kc, 5192:5256], wt.p("ikdup").res), hT[:, kc, 0:ntok], start=(kc == 0), stop=(kc == 7))
            sg = stF.next()
            kb.evac(sg[:, 0:ntok], ps[:, 0:ntok])
            kb.dma("sp", io["KIT"][:, s0 * 128:s0 * 128 + ntok], sg[:, 0:ntok])
            for sl in range(ns):
                s = s0 + sl
                rows = slice(s * 128, (s + 1) * 128)
                lhs = lambda kc: hT[:, kc, sl * 128:(sl + 1) * 128]

                def proj(c0, n):
                    ps = pM.next()
                    res = wcols(c0, n)
                    for kc in range(8):
                        kb.mm(ps[:, 0:n], lhs(kc), V(wt.t[:, kc, c0:c0 + n], res), start=(kc == 0), stop=(kc == 7))
                    return ps
                ps = proj(C_AV, 512)
                sg = stM.next()
                kb.evac(sg[:], ps[:])
                kb.dma("sp", io["V"][rows, :], sg[:])
                ps = proj(C_IW, 8)
                sw = stW.next()
                kb.copy("dve", sw[:], ps[:, 0:8])
                kb.dma("sp", io["IW"][rows, :], sw[:])
                for (nm, c0, ct, stb) in (("BQ", C_BQ, cosq, sinq), ("BK", C_BK, cosk, sink)):
                    ps = proj(c0, 512)
                    sg = stM.next()
                    x1 = V(ps.t[:, :].rearrange("p (h t f) -> p h t f", h=4, t=2)[:, :, 0, :], [ps.r])
                    x2 = V(ps.t[:, :].rearrange("p (h t f) -> p h t f", h=4, t=2)[:, :, 1, :], [ps.r])
                    o1 = V(sg.t[:, :].rearrange("p (h t f) -> p h t f", h=4, t=2)[:, :, 0, :], [sg.r])
                    o2 = V(sg.t[:, :].rearrange("p (h t f) -> p h t f", h=4, t=2)[:, :, 1, :], [sg.r])
                    cb = V(ct.t[:, s, :].unsqueeze(1).to_broadcast([128, 4, 64]), [ct.r])
                    sbb = V(stb.t[:, s, :].unsqueeze(1).to_broadcast([128, 4, 64]), [stb.r])
                    a, b = ra.next(), rb.next()
                    kb.tt("dve", a[:], x1, cb, ALU.mult)
                    kb.tt("dve", b[:], x2, sbb, ALU.mult)
                    kb.tt("pool", o1, a[:], b[:], ALU.subtract)
                    a, b = ra.next(), rb.next()
                    kb.tt("dve", a[:], x2, cb, ALU.mult)
                    kb.tt("dve", b[:], x1, sbb, ALU.mult)
                    kb.tt("pool", o2, a[:], b[:], ALU.add)
                    kb.dma("sp", io[nm][rows, :], sg[:])
                for j in range(2):
                    ps = proj(C_BV + 512 * j, 512)
                    sg = stM.next()
                    kb.evac(sg[:], ps[:])
                    kb.dma("sp", io["BV"][rows, 512 * j:512 * (j + 1)], sg[:])
                for j in range(2):
                    ps = proj(C_BG + 512 * j, 512)
                    sg = stG.next()
                    kb.evac(sg[:], ps[:])
                    kb.dma("sp", io["BG"][rows, 512 * j:512 * (j + 1)], sg[:])
        kb.P.flush()


def build_launch(phases, in_specs, out_specs):
    nc = bass.Bass("TRN2", target_bir_lowering=False)
    with ExitStack() as st:
        kb = KB(nc, st)
        io = {}
        for (name, shape, dt) in in_specs:
            io[name] = kb.dram(name, shape, dt, kind="ExternalInput")
        for (name, shape, dt) in out_specs:
            io[name] = kb.dram(name, shape, dt, kind="ExternalOutput")
        for ph in phases:
            ph(kb, io)
        stats = (kb.P.n_ins, kb.P.n_wait)
    return nc, stats


SPEC_A_IN = [("xs", [NT, D], F32), ("w_in", [D, 5192], F32), ("gmix0", [128, D], F32), ("ident", [128, 128], BF16),
             ("cosq", [NT, 64], F32), ("sinq", [NT, 64], F32), ("cosk", [NT, 64], F32), ("sink", [NT, 64], F32)]
SPEC_A_OUT = [("QT", [512, NT], BF16), ("KT", [512, NT], BF16), ("QIT", [512, NT], BF16), ("KIT", [128, NT], BF16),
              ("V", [NT, 512], BF16), ("IW", [NT, 8], F32), ("BQ", [NT, 512], BF16), ("BK", [NT, 512], BF16),
              ("BV", [NT, 1024], BF16), ("BG", [NT, 1024], F32)]


def t5_bucket_np(rel):
    n = np.maximum(rel, 0)
    max_exact = 16
    with np.errstate(divide="ignore"):
        large = max_exact + (np.log(np.maximum(n, 1).astype(np.float32) / max_exact)
                             / math.log(128 / max_exact) * (32 - max_exact)).astype(np.int32)
    large = np.minimum(large, 31)
    return np.where(n < max_exact, n, large)


def dsa_consts(c, rel_bias):
    q = np.arange(128)[None, :]
    k = np.arange(128)[:, None]
    eb = np.full((128, 8, 3, 128), NEG, np.float32)
    for j in range(3):
        dB = (2 * 0 + 1 - c) - (2 * 0 - 1 + j)
        rel = dB * 128 + q - k
        vis = rel >= 0
        bk = t5_bucket_np(rel)
        for h in range(8):
            eb[:, h, j, :] = np.where(vis, rel_bias[bk, h], NEG)
    pen = np.zeros((128, 2, 128), np.float32)
    for j in range(2):
        dB = (1 - c) - j
        rel = dB * 128 + q.T - k.T
        pen[:, j, :] = np.where(rel >= 0, 0.0, -BIG)
    metaM = np.zeros((128, 128), np.float32)
    metaM[112:, :] = 1.0
    return {"ebraw": eb.reshape(128, 8 * 3 * 128), "pen": pen.reshape(128, 256),
            "metaM": metaM.astype(NPBF), "b31": rep128(rel_bias[31])}


NBIS = 26
KEEP = 240.0


def phase_dsa(kb, io, aout):
    with ExitStack() as st:
        sb = lambda n, s, d: kb.sb(st, n, s, d)
        KT = sb("KT", [128, 4, NG], BF16)
        KIT = sb("KIT", [128, NG], BF16)
        VA = sb("VA", [128, NB, 8, 65], BF16)
        ident = sb("ident", [128, 128], BF16)
        EB = sb("EB", [128, 8, 3, 128], F32)
        ebraw = sb("ebraw", [128, 8, 3 * 128], F32)
        nb31 = sb("nb31", [128, 8], F32)
        pen = sb("pen", [128, 256], F32)
        pw = sb("pw", [128, NBIS], F32)
        acc = sb("acc", [128, 32 * 128], F32)
        sel = sb("sel", [128, 32 * 128], BF16)
        selT = sb("selT", [128, NB, 128], BF16)
        qts = Ring([sb(f"qt{i}", [128, 4, 128], BF16) for i in range(2)])
        qits = Ring([sb(f"qit{i}", [128, 4, 128], BF16) for i in range(2)])
        iws = Ring([sb(f"iw{i}", [128, 8], F32) for i in range(2)])
        absw = Ring([sb(f"absw{i}", [128, 8], F32) for i in range(2)])
        sgnw = Ring([sb(f"sgnw{i}", [128, 8], F32) for i in range(2)])
        rt = Ring([sb(f"rt{i}", [128, 512], F32) for i in range(3)])
        et = Ring([sb(f"et{i}", [128, 4, 128], BF16) for i in range(3)])
        pt = Ring([sb(f"pt{i}", [128, 4, 128], BF16) for i in range(3)])
        junk = sb("junkb", [128, 32 * 128], BF16)
        sm = {n: sb(n, [128, 1], F32) for n in ("mx", "lo", "w0", "mid", "cnt", "tstep")}
        wk = sb("wk", [128, NBIS], F32)
        den = Ring([sb(f"den{i}", [128, 4], F32) for i in range(2)])
        psS = Ring([kb.ps(st, f"psS{i}", [128, 512], F32) for i in range(2)])
        psZ = Ring([kb.ps(st, f"psZ{i}", [128, 4, 128], F32) for i in range(2)])
        psO = Ring([kb.ps(st, f"psO{i}", [128, 4, 128], F32) for i in range(2)])
        psT = Ring([kb.ps(st, f"psT{i}", [128, 8, 128], BF16) for i in range(2)])

        kb.dma("sp", ident[:], io["ident"][:])
        for cch in range(4):
            kb.dma("sp", KT.p(cch)[:, cch, :], io["KTg"][cch * 128:(cch + 1) * 128, :])
        kb.dma("sp", KIT[:], io["KITg"][:])
        for B in range(NB):
            kb.dma("sp" if B % 2 else "qact", VA.p(B)[:, B, :, 0:64],
                   V(io["Vg"].t[B * 128:(B + 1) * 128, :].rearrange("p (h d) -> p h d", h=8), []))
            kb.memset("pool", VA.p(B)[:, B, :, 64:65], 1.0)
        kb.dma("sp", ebraw[:], io["ebraw"][:])
        kb.dma("sp", nb31[:], io["b31"][:])
        kb.dma("sp", pen[:], io["pen"][:])
        kb.dma("sp", pw[:], io["pw"][:])
        kb.dma("sp", selT.p("meta")[:, 1, :], io["metaM"][:])
        kb.ts("dve", nb31[:], nb31[:], -1.0, None, ALU.mult)
        for h in range(8):
            kb.act(V(EB.t[:, h, :, :].rearrange("p a q -> p (a q)"), [EB.r]), ebraw[:, h, :], AF.Exp, bias=nb31[:, h:h + 1])
        VAall = lambda ap: V(ap, VA.all().res)
        KTall = lambda ap: V(ap, KT.all().res)
        import os
        STOP = os.environ.get("DSA_STOP", "")
        if STOP == "loads":
            kb.memset("dve", aout[:], 0.0)
            kb.P.flush()
            return

        for i in range(NS):
            nkb = 2 * i + 2
            cols = slice(i * 128, (i + 1) * 128)
            qt = qts.next()
            kb.dma("sp", qt[:], V(io["QT"].t[:, cols].rearrange("(c p) n -> p c n", p=128), []))
            ncand = 2 * i
            if ncand > 0:
                qit = qits.next()
                kb.dma("sp", qit[:], V(io["QIT"].t[:, cols].rearrange("(c p) n -> p c n", p=128), []))
                iw = iws.next()
                kb.dma("sp", iw[:], io["IW"][cols, :])
                aw, sg = absw.next(), sgnw.next()
                kb.act(aw[:], iw[:], AF.Abs)
                kb.act(sg[:], iw[:], AF.Sign)
                ncol = ncand * 128
                for k0 in range(0, ncol, 512):
                    n = min(512, ncol - k0)
                    for h in range(8):
                        ps = psS.next()
                        hp = slice((h % 2) * 64, (h % 2) * 64 + 64)
                        kb.mm(ps[:, 0:n], qit[hp, h // 2, :], KIT[hp, 256 + k0:256 + k0 + n])
                        r = rt.next()
                        kb.act(r[:, 0:n], ps[:, 0:n], AF.Relu, scale=aw[:, h:h + 1])
                        if h == 0:
                            kb.ts("dve", acc[:, k0:k0 + n], r[:, 0:n], sg[:, 0:1], None, ALU.mult)
                        else:
                            kb.stt("dve", acc[:, k0:k0 + n], r[:, 0:n], sg[:, h:h + 1], acc[:, k0:k0 + n], ALU.mult, ALU.add)
                if STOP == "score":
                    continue
                A = acc[:, 0:ncol]
                r_, w_ = kb._rw([sm["mx"][:]], [A])
                kb.P.op("dve", lambda e, A=A: e.tensor_reduce(out=sm["mx"].t[:], in_=A.ap, axis=AX.X, op=ALU.max,
                                                              apply_absolute_value=True), reads=r_, writes=w_)
                kb.tt("dve", acc[:, ncol - 256:ncol], acc[:, ncol - 256:ncol], pen[:], ALU.add)
                kb.ts("dve", sm["lo"][:], sm["mx"][:], -1.0, -1.0, ALU.mult, ALU.add)
                kb.ts("dve", sm["w0"][:], sm["mx"][:], 2.0, 2.0, ALU.mult, ALU.add)
                kb.ts("dve", wk[:], pw[:], sm["w0"][:, 0:1], None, ALU.mult)
                for it in range(NBIS):
                    kb.tt("dve", sm["mid"][:], sm["lo"][:], wk[:, it:it + 1], ALU.add)
                    kb.ts("dve", junk[:, 0:ncol], A, sm["mid"][:, 0:1], 0.0, ALU.is_ge, ALU.add, accum=sm["cnt"][:])
                    kb.stt("dve", sm["tstep"][:], sm["cnt"][:], KEEP - 0.5, wk[:, it:it + 1], ALU.is_ge, ALU.mult)
                    kb.tt("dve", sm["lo"][:], sm["lo"][:], sm["tstep"][:], ALU.add)
                kb.ts("dve", sel[:, 0:ncol], A, sm["lo"][:, 0:1], None, ALU.is_ge)
                for b0 in range(0, ncand, 4):
                    nb = min(4, ncand - b0)
                    pT = psT.next()
                    for j in range(nb):
                        kb.tr(pT[:, j, :], sel[:, (b0 + j) * 128:(b0 + j + 1) * 128], ident[:])
                    kb.evac(selT.p("c")[:, 2 + b0:2 + b0 + nb, :], pT[:, 0:nb, :])
            if STOP in ("score", "thr"):
                continue
            for hg in range(2):
                po = psO.next()
                hp = slice(hg * 64, hg * 64 + 64)
                for B in range(1, nkb):
                    pz = psZ.next()
                    for hh in range(4):
                        kb.mm(pz[:, hh, :], KTall(KT.t[hp, hh, B * 128:(B + 1) * 128]), qt[hp, hh, :])
                    e = et.next()
                    kb.act(e[:], pz[:], AF.Exp, scale=0.125)
                    p = pt.next()
                    mres = selT.p("meta").res if B == 1 else selT.p("c").res
                    kb.tt("dve", p[:], e[:], V(selT.t[:, B, :].unsqueeze(1).to_broadcast([128, 4, 128]), mres), ALU.mult)
                    j = B - (2 * i - 1)
                    if j >= 0:
                        kb.tt("pool", p[:], p[:], V(EB.t[:, hg::2, j, :], [EB.r]), ALU.mult)
                    for hh in range(4):
                        if STOP == "noav":
                            break
                        h = 2 * hh + hg
                        kb.mm(po[:, hh, 0:65], p[:, hh, :], VAall(VA.t[:, B, h, :]), start=(B == 1 and hh == 0), stop=(B == nkb - 1))
                if STOP == "noav":
                    continue
                dn = den.next()
                kb.ts("dve", dn[:], po[:, :, 64], 1e-30, None, ALU.max)
                kb.recip(dn[:], dn[:])
                kb.tt("dve", V(aout.t[:, i, :].rearrange("p (h d) -> p h d", h=8)[:, hg::2, :], [aout.r]),
                      po[:, :, 0:64], V(dn.t[:, :].unsqueeze(2).to_broadcast([128, 4, 64]), [dn.r]), ALU.mult)
        kb.P.flush()


RET_LG = [math.log(1.0 - 2.0 ** (-5.0 - h)) for h in range(4)]
RET_G = [math.exp(128.0 * lg) for lg in RET_LG]


def ret_consts(c):
    k = np.arange(128, dtype=np.float64)[:, None]
    q = np.arange(128, dtype=np.float64)[None, :]
    DA = np.zeros((128, 4, 128), np.float32)
    DB = np.zeros((128, 4, 128), np.float32)
    XI = np.zeros((128, 4, 128), np.float32)
    ZT = np.zeros((128, 4), np.float32)
    for h in range(4):
        lg = RET_LG[h]
        caus = np.where(q >= k, np.exp(np.maximum(q - k, 0) * lg), 0.0)
        if c == 0:
            DA[:, h, :] = np.exp((q + 128 - k) * lg)
            DB[:, h, :] = caus
            XI[:, h, :] = np.exp((q + 129) * lg)
        else:
            DA[:, h, :] = caus
            XI[:, h, :] = np.exp((q + 1) * lg)
        ZT[:, h] = np.exp((127 - k[:, 0]) * lg)
    return {"DA": DA.reshape(128, 512), "DB": DB.reshape(128, 512), "XI": XI.reshape(128, 512), "ZT": ZT}


def phase_ret_out(kb, io, aout):
    with ExitStack() as st:
        sb = lambda n, s, d: kb.sb(st, n, s, d)
        ident = sb("ident", [128, 128], BF16)
        wo = sb("wo", [128, 12, 1024], BF16)
        gain = sb("gain", [128, 1024], F32)
        DA = sb("DA", [128, 4, 128], F32)
        DB = sb("DB", [128, 4, 128], F32)
        XI = sb("XI", [128, 4, 128], F32)
        ZT = sb("ZT", [128, 4], F32)
        R = sb("R", [128, 4, 256], F32)
        Rbf = sb("Rbf", [128, 4, 256], BF16)
        bqs = Ring([sb(f"bq{i}", [128, 512], BF16) for i in range(2)])
        bks = Ring([sb(f"bk{i}", [128, 512], BF16) for i in range(4)])
        bvs = Ring([sb(f"bv{i}", [128, 1024], BF16) for i in range(4)])
        kzs = Ring([sb(f"kz{i}", [128, 4, 128], BF16) for i in range(2)])
        bkTs = Ring([sb(f"bkT{i}", [128, 4, 128], BF16) for i in range(4)])
        bqTs = Ring([sb(f"bqT{i}", [128, 4, 128], BF16) for i in range(2)])
        qxs = Ring([sb(f"qx{i}", [128, 4, 128], BF16) for i in range(2)])
        sAs = Ring([sb(f"sA{i}", [128, 4, 128], BF16) for i in range(4)])
        bgs = Ring([sb(f"bg{i}", [128, 1024], F32) for i in range(2)])
        sgt = sb("sgt", [128, 1024], F32)
        rn = sb("rn", [128, 1024], F32)
        cats = Ring([sb(f"cat{i}", [128, 1536], BF16) for i in range(2)])
        catTs = Ring([sb(f"catT{i}", [128, 12, 128], BF16) for i in range(2)])
        xss = Ring([sb(f"xs{i}", [128, 1024], F32) for i in range(2)])
        hos = Ring([sb(f"ho{i}", [128, 512], F32) for i in range(2)])
        junk = sb("junkr", [128, 256], F32)
        stt_ = {n: sb(n, [128, 4], F32) for n in ("ssum", "ssq", "mean", "msq", "var", "rstd")}
        psT = Ring([kb.ps(st, f"psT{i}", [128, 8, 128], BF16) for i in range(2)])
        psZ = Ring([kb.ps(st, f"psZ{i}", [128, 4, 128], F32) for i in range(2)])
        psR = kb.ps(st, "psR", [128, 4, 256], F32)
        psX = Ring([kb.ps(st, f"psX{i}", [128, 512], F32) for i in range(2)])

        kb.dma("sp", ident[:], io["ident"][:])
        kb.dma("sp", gain[:], io["gngain"][:])
        for nm, t_ in (("DA", DA), ("DB", DB), ("XI", XI)):
            kb.dma("sp", t_[:], V(io[nm].t.rearrange("p (h q) -> p h q", h=4), []))
        kb.dma("sp", ZT[:], io["ZT"][:])
        for half in range(2):
            src = io["w_out0"].t[:, half * 512:(half + 1) * 512].rearrange("(kc p) m -> p kc m", p=128)
            kb.dma("qpool", wo.p(half)[:, :, half * 512:(half + 1) * 512], V(src, []))
        kb.memset("dve", R[:], 0.0)
        kb.memset("dve", Rbf[:], 0.0)

        def load_block(B):
            bk, bv = bks.next(), bvs.next()
            kb.dma("sp", bk[:], io["BKg"][B * 128:(B + 1) * 128, :])
            kb.dma("sp", bv[:], io["BVg"][B * 128:(B + 1) * 128, :])
            return bk, bv

        def transpose4(src):
            pT = psT.next()
            for h in range(4):
                kb.tr(pT[:, h, :], src[:, h * 128:(h + 1) * 128], ident[:])
            return pT

        for i in range(NS):
            rows = slice(i * 128, (i + 1) * 128)
            bq = bqs.next()
            kb.dma("sp", bq[:], io["BQ"][rows, :])
            blkA = load_block(2 * i)
            blkB = load_block(2 * i + 1)
            pT = transpose4(bq)
            bqT = bqTs.next()
            kb.evac(bqT[:], pT[:, 0:4, :])
            qx = qxs.next()
            kb.tt("pool", qx[:], bqT[:], XI[:], ALU.mult)
            ss = []
            for (bk, bv), Dm in ((blkA, DA), (blkB, DB)):
                pT = transpose4(bk)
                bkT = bkTs.next()
                kb.evac(bkT[:], pT[:, 0:4, :])
                pz = psZ.next()
                for h in range(4):
                    kb.mm(pz[:, h, :], bkT[:, h, :], bqT[:, h, :])
                sA = sAs.next()
                kb.tt("dve", sA[:], pz[:], Dm[:], ALU.mult)
                ss.append(sA)
            for h in range(4):
                first = (h % 2 == 0)
                kb.mm(psR[:, h, :], ss[0][:, h, :], blkA[1][:, h * 256:(h + 1) * 256], start=first, stop=False)
                kb.mm(psR[:, h, :], ss[1][:, h, :], blkB[1][:, h * 256:(h + 1) * 256], start=False, stop=(i == 0))
                if i > 0:
                    kb.mm(psR[:, h, :], qx[:, h, :], Rbf[:, h, :], start=False, stop=True)
            for h in range(4):
                kb.act(junk[:], psR[:, h, :], AF.Copy, accum=stt_["ssum"][:, h:h + 1])
                kb.act(junk[:], psR[:, h, :], AF.Square, accum=stt_["ssq"][:, h:h + 1])
            kb.ts("dve", stt_["mean"][:], stt_["ssum"][:], 1.0 / 256, None, ALU.mult)
            kb.tt("dve", stt_["msq"][:], stt_["mean"][:], stt_["mean"][:], ALU.mult)
            kb.stt("dve", stt_["var"][:], stt_["ssq"][:], 1.0 / 256, stt_["msq"][:], ALU.mult, ALU.subtract)
            kb.act(stt_["rstd"][:], stt_["var"][:], AF.Ln, bias=1e-6)
            kb.act(stt_["rstd"][:], stt_["rstd"][:], AF.Exp, scale=-0.5)
            for h in range(4):
                kb.ts("dve", rn[:, h * 256:(h + 1) * 256], psR[:, h, :], stt_["mean"][:, h:h + 1], stt_["rstd"][:, h:h + 1],
                      ALU.subtract, ALU.mult)
            bg = bgs.next()
            kb.dma("sp", bg[:], io["BG"][rows, :])
            kb.act(sgt[:], bg[:], AF.Silu)
            kb.tt("pool", sgt[:], sgt[:], gain[:], ALU.mult)
            cat = cats.next()
            kb.tt("dve", cat[:, 512:1536], rn[:], sgt[:], ALU.mult)
            kb.copy("pool", cat[:, 0:512], aout[:, i, :])
            catT = catTs.next()
            for g0 in (0, 8):
                n = min(8, 12 - g0)
                pT = psT.next()
                for j in range(n):
                    kb.tr(pT[:, j, :], cat[:, (g0 + j) * 128:(g0 + j + 1) * 128], ident[:])
                kb.evac(catT[:, g0:g0 + n, :], pT[:, 0:n, :])
            xs = xss.next()
            kb.dma("sp", xs[:], io["xs"][rows, :])
            for half in range(2):
                pw = psX.next()
                for kc in range(12):
                    kb.mm(pw[:], catT[:, kc, :], wo.p(half)[:, kc, half * 512:(half + 1) * 512], start=(kc == 0), stop=(kc == 11))
                ho = hos.next()
                kb.tt("dve", ho[:], pw[:], xs[:, half * 512:(half + 1) * 512], ALU.add)
                kb.dma("sp", io["H1"][rows, half * 512:(half + 1) * 512], ho[:])
            if i < NS - 1:
                for (bk, bv) in (blkA, blkB):
                    kz = kzs.next()
                    kb.tt("pool", kz[:], V(bk.t[:, :].rearrange("p (h d) -> p h d", h=4), [bk.r]),
                          V(ZT.t[:, :].unsqueeze(2).to_broadcast([128, 4, 128]), [ZT.r]), ALU.mult)
                    for hp2 in range(2):
                        px = psX.next()
                        for hh in range(2):
                            h = hp2 * 2 + hh
                            kb.mm(px[:, hh * 256:(hh + 1) * 256], kz[:, h, :], bv[:, h * 256:(h + 1) * 256],
                                  start=(hh == 0), stop=True)
                        for hh in range(2):
                            h = hp2 * 2 + hh
                            kb.stt("dve", R[:, h, :], R[:, h, :], float(RET_G[h]), px[:, hh * 256:(hh + 1) * 256], ALU.mult, ALU.add)
                kb.copy("act", Rbf[:], R[:])
        kb.P.flush()


def phase_mixer0(kb, io):
    with ExitStack() as st:
        aout = kb.sb(st, "aout", [128, NS, 512], BF16)
        phase_dsa(kb, io, aout)
        phase_ret_out(kb, io, aout)


SPEC_B_IN = [("QT", [512, NT], BF16), ("QIT", [512, NT], BF16), ("IW", [NT, 8], F32),
             ("KTg", [512, NG], BF16), ("KITg", [128, NG], BF16), ("Vg", [NG, 512], BF16),
             ("ident", [128, 128], BF16), ("ebraw", [128, 8 * 3 * 128], F32), ("pen", [128, 256], F32),
             ("metaM", [128, 128], BF16), ("b31", [128, 8], F32), ("pw", [128, NBIS], F32),
             ("BQ", [NT, 512], BF16), ("BKg", [NG, 512], BF16), ("BVg", [NG, 1024], BF16), ("BG", [NT, 1024], F32),
             ("xs", [NT, D], F32), ("w_out0", [1536, 1024], F32), ("gngain", [128, 1024], F32),
             ("DA", [128, 512], F32), ("DB", [128, 512], F32), ("XI", [128, 512], F32), ("ZT", [128, 4], F32)]
SPEC_B_OUT = [("H1", [NT, D], F32)]


FCG = [(0, 4), (4, 4), (8, 4), (12, 4), (16, 4), (20, 2)]


def phase_ffn(kb, io, L, hin, halo, hout, final_out=None):
    with ExitStack() as st:
        sb = lambda n, s, d: kb.sb(st, n, s, d)
        ident = sb("ident", [128, 128], BF16)
        g32 = sb("g32", [128, 1024], F32)
        gfin = sb("gfin", [128, 1024], F32) if final_out else None
        wd = sb("wd", [128, NFC, 1024], BF16)
        cw = sb("cw", [128, NFC, 3], F32)
        cb = sb("cb", [128, NFC], F32)
        wus = Ring([sb(f"wu{i}", [128, 8, 512], BF16) for i in range(2)])
        wgs = Ring([sb(f"wg{i}", [128, 8, 512], BF16) for i in range(2)])
        xg = sb("xg", [128, 4, 1024], F32)
        xh = sb("xh", [128, 1024], F32)
        hTs = Ring([sb(f"hT{i}", [128, 8, 512], BF16) for i in range(2)])
        hTh = sb("hTh", [128, 8, 128], BF16)
        actT = sb("actT", [128, NFC, 512], BF16)
        gxs = Ring([sb(f"gx{i}", [128, 4, 130], F32) for i in range(2)])
        t1s = Ring([sb(f"t1_{i}", [128, 4, 128], F32) for i in range(2)])
        t2s = Ring([sb(f"t2_{i}", [128, 4, 128], F32) for i in range(2)])
        sls = Ring([sb(f"sl_{i}", [128, 4, 128], F32) for i in range(2)])
        hns = Ring([sb(f"hnw{i}", [128, 1024], F32) for i in range(2)])
        fos = Ring([sb(f"fo{i}", [128, 1024], F32) for i in range(2)])
        scr = {"junk": sb("junk", [128, 1024], F32), "ssq": sb("ssq", [128, 1], F32),
               "rstd": sb("rstd", [128, 1], F32), "hn": sb("hn", [128, 1024], BF16),
               "pT": Ring([kb.ps(st, f"pT{i}", [128, 8, 128], BF16) for i in range(2)])}
        pA = Ring([kb.ps(st, f"pA{i}", [128, 512], F32) for i in range(4)])
        pH = Ring([kb.ps(st, f"pH{i}", [128, 512], F32) for i in range(2)])

        kb.dma("sp", ident[:], io["ident"][:])
        kb.dma("sp", g32[:], io[f"gffn{L}"][:])
        kb.ts("dve", g32[:], g32[:], 32.0, None, ALU.mult)
        if final_out:
            kb.dma("sp", gfin[:], io["gfinal"][:])
            kb.ts("dve", gfin[:], gfin[:], 32.0, None, ALU.mult)
        kb.dma("sp", cw[:], V(io[f"cwT{L}"].t.rearrange("p (f j) -> p f j", j=3), []))
        kb.dma("sp", cb[:], io[f"cbT{L}"][:])
        for fc in range(NFC):
            kb.dma("qpool", wd.p(fc)[:, fc, :], io[f"w_down{L}"][fc * 128:(fc + 1) * 128, :])
        wdall = lambda ap: V(ap, wd.all().res)
        kb.memset("dve", xh[:], 0.0)

        for (s0, ns) in GROUPS:
            ntok = ns * 128
            hT = hTs.next()
            for sl in range(ns):
                s = s0 + sl
                kb.dma("sp", xg.p(sl)[:, sl, :], io[hin][s * 128:(s + 1) * 128, :])
                emit_norm_T(kb, xg.p(sl)[:, sl, :], g32[:], ident[:], hT[:, :, sl * 128:(sl + 1) * 128], scr)
            kb.dma("sp", xh[0:2 * ns, :], io[halo][2 * s0:2 * s0 + 2 * ns, :])
            emit_norm_T(kb, xh[:], g32[:], ident[:], hTh[:], scr)
            for (f0, nf) in FCG:
                wu, wg = wus.next(), wgs.next()
                ncol = nf * 128
                for (wt_, nm) in ((wu, f"w_up{L}"), (wg, f"w_gate{L}")):
                    src = io[nm].t[:, f0 * 128:f0 * 128 + ncol].rearrange("(kc p) m -> p kc m", p=128)
                    kb.dma("qpool", wt_[:, :, 0:ncol], V(src, []))
                for fl in range(nf):
                    fc = f0 + fl
                    wc = slice(fl * 128, (fl + 1) * 128)
                    pu, pg, ph = pA.next(), pA.next(), pH.next()
                    for kc in range(8):
                        kb.mm(pu[:, 0:ntok], wu[:, kc, wc], hT[:, kc, 0:ntok], start=(kc == 0), stop=(kc == 7))
                    for kc in range(8):
                        kb.mm(pg[:, 0:ntok], wg[:, kc, wc], hT[:, kc, 0:ntok], start=(kc == 0), stop=(kc == 7))
                    for kc in range(8):
                        kb.mm(ph[:, 0:2 * ns], wg[:, kc, wc], hTh[:, kc, 0:2 * ns], start=(kc == 0), stop=(kc == 7))
                    gx, t1, t2, sl_ = gxs.next(), t1s.next(), t2s.next(), sls.next()
                    v3 = lambda t_, a, b: V(t_.t[:, 0:ns, a:b], [t_.r])
                    pg3 = V(pg.t[:, 0:ntok].rearrange("p (s t) -> p s t", s=ns), [pg.r])
                    pu3 = V(pu.t[:, 0:ntok].rearrange("p (s t) -> p s t", s=ns), [pu.r])
                    ph3 = V(ph.t[:, 0:2 * ns].rearrange("p (s t) -> p s t", s=ns), [ph.r])
                    kb.copy("act", v3(gx, 2, 130), pg3)
                    kb.copy("act", v3(gx, 0, 2), ph3)
                    kb.ts("dve", v3(t1, 0, 128), v3(gx, 2, 130), cw[:, fc, 2:3], cb[:, fc:fc + 1], ALU.mult, ALU.add)
                    kb.stt("dve", v3(t2, 0, 128), v3(gx, 1, 129), cw[:, fc, 1:2], v3(t1, 0, 128), ALU.mult, ALU.add)
                    kb.stt("dve", v3(t1, 0, 128), v3(gx, 0, 128), cw[:, fc, 0:1], v3(t2, 0, 128), ALU.mult, ALU.add)
                    kb.act(v3(sl_, 0, 128), v3(t1, 0, 128), AF.Silu)
                    kb.tt("dve", V(actT.t[:, fc, 0:ntok].rearrange("p (s t) -> p s t", s=ns), actT.p(fc)[:].res), v3(sl_, 0, 128), pu3, ALU.mult)
            actall = lambda ap: V(ap, actT.all().res)
            for sl in range(ns):
                s = s0 + sl
                hn_ = hns.next()
                for half in range(2):
                    pd = pA.next()
                    for fc in range(NFC):
                        kb.mm(pd[:], actall(actT.t[:, fc, sl * 128:(sl + 1) * 128]), wdall(wd.t[:, fc, half * 512:(half + 1) * 512]),
                              start=(fc == 0), stop=(fc == NFC - 1))
                    kb.tt("dve", hn_[:, half * 512:(half + 1) * 512], pd[:], xg.p(sl)[:, sl, half * 512:(half + 1) * 512], ALU.add)
                kb.dma("sp", io[hout][s * 128:(s + 1) * 128, :], hn_[:])
                if final_out:
                    kb.act(scr["junk"][:], hn_[:], AF.Square, accum=scr["ssq"][:])
                    kb.act(scr["rstd"][:], scr["ssq"][:], AF.Ln, bias=1024 * 1e-6)
                    kb.act(scr["rstd"][:], scr["rstd"][:], AF.Exp, scale=-0.5)
                    fo = fos.next()
                    kb.stt("dve", fo[:], hn_[:], scr["rstd"][:, 0:1], gfin[:], ALU.mult, ALU.mult)
                    kb.dma("sp", io[final_out][s * 128:(s + 1) * 128, :], fo[:])
        kb.P.flush()


def phase_proj1(kb, io, hin):
    with ExitStack() as st:
        sb = lambda n, s, d: kb.sb(st, n, s, d)
        wt = sb("wt1", [128, 8, 3072], BF16)
        ident = sb("ident", [128, 128], BF16)
        g32 = sb("g32", [128, 1024], F32)
        xts = Ring([sb(f"xt{i}", [128, 1024], F32) for i in range(2)])
        hnT = Ring([sb(f"hnT{i}", [128, 8, 512], BF16) for i in range(2)])
        scr = {"junk": sb("junk", [128, 1024], F32), "ssq": sb("ssq", [128, 1], F32),
               "rstd": sb("rstd", [128, 1], F32), "hn": sb("hn", [128, 1024], BF16),
               "pT": Ring([kb.ps(st, f"pT{i}", [128, 8, 128], BF16) for i in range(2)])}
        pF = Ring([kb.ps(st, f"pF{i}", [128, 512], F32) for i in range(3)])
        pM = Ring([kb.ps(st, f"pM{i}", [128, 512], F32) for i in range(3)])
        stF = Ring([sb(f"stF{i}", [128, 512], BF16) for i in range(3)])
        stM = Ring([sb(f"stM{i}", [128, 512], BF16) for i in range(3)])
        kb.dma("sp", ident[:], io["ident"][:])
        kb.dma("sp", g32[:], io["gmix1"][:])
        kb.ts("dve", g32[:], g32[:], 32.0, None, ALU.mult)
        for c0 in range(0, 3072, 512):
            load_w_cast(kb, wt.p(c0 // 512), io["w_in1"], c0, 512, dcol0=c0)
        for (s0, ns) in GROUPS:
            ntok = ns * 128
            hT = hnT.next()
            for sl in range(ns):
                s = s0 + sl
                xt = xts.next()
                kb.dma("sp", xt[:], io[hin][s * 128:(s + 1) * 128, :])
                emit_norm_T(kb, xt[:], g32[:], ident[:], hT[:, :, sl * 128:(sl + 1) * 128], scr)
            for (nm, cbase) in (("QT1", 0), ("KT1", 1024)):
                for j in range(8):
                    c0 = cbase + 128 * j
                    ps = pF.next()
                    for kc in range(8):
                        kb.mm(ps[:, 0:ntok], wt.p(c0 // 512)[:, kc, c0:c0 + 128], hT[:, kc, 0:ntok], start=(kc == 0), stop=(kc == 7))
                    sg = stF.next()
                    kb.evac(sg[:, 0:ntok], ps[:, 0:ntok])
                    kb.dma("sp", io[nm][j * 128:(j + 1) * 128, s0 * 128:s0 * 128 + ntok], sg[:, 0:ntok])
            for sl in range(ns):
                s = s0 + sl
                for j in range(2):
                    c0 = 2048 + 512 * j
                    ps = pM.next()
                    for kc in range(8):
                        kb.mm(ps[:], hT[:, kc, sl * 128:(sl + 1) * 128], wt.p(c0 // 512)[:, kc, c0:c0 + 512], start=(kc == 0), stop=(kc == 7))
                    sg = stM.next()
                    kb.evac(sg[:], ps[:])
                    kb.dma("sp", io["V1"][s * 128:(s + 1) * 128, 512 * j:512 * (j + 1)], sg[:])
        kb.P.flush()


def ffn_specs(L):
    return [(f"w_up{L}", [D, DFF], F32), (f"w_gate{L}", [D, DFF], F32), (f"w_down{L}", [DFF, D], F32),
            (f"cwT{L}", [128, NFC * 3], F32), (f"cbT{L}", [128, NFC], F32), (f"gffn{L}", [128, D], F32)]


SPEC_C_IN = [("H1", [NT, D], F32), ("HALO1", [NS * 2, D], F32), ("ident", [128, 128], BF16),
             ("gmix1", [128, D], F32), ("w_in1", [D, 3072], F32)] + ffn_specs(0)
SPEC_C_OUT = [("H2", [NT, D], F32), ("QT1", [1024, NT], BF16), ("KT1", [1024, NT], BF16), ("V1", [NT, 1024], BF16)]


def launch_C_phases():
    return [lambda kb, io: phase_ffn(kb, io, 0, "H1", "HALO1", "H2"), lambda kb, io: phase_proj1(kb, io, "H2")]


def ffn_host_inputs(inp, L):
    cw = inp["ffn_conv_w"][L]
    cwT = np.ascontiguousarray(cw.reshape(3, NFC, 128).transpose(2, 1, 0).reshape(128, NFC * 3))
    cbT = np.ascontiguousarray(inp["ffn_conv_b"][L].reshape(NFC, 128).T)
    return {f"w_up{L}": inp["ffn_w_up"][L], f"w_gate{L}": inp["ffn_w_gate"][L], f"w_down{L}": inp["ffn_w_down"][L],
            f"cwT{L}": cwT, f"cbT{L}": cbT, f"gffn{L}": rep128(inp["norm_ffn"][L])}


def make_halo(h0, h1):
    g = merge_pair(h0, h1, axis=0)
    out = []
    for c in range(2):
        hl = np.zeros((NS * 2, D), np.float32)
        for i, B in enumerate(own_blocks(c)):
            if B > 0:
                hl[2 * i:2 * i + 2] = g[B * 128 - 2:B * 128]
        out.append(hl)
    return out


def sb_consts(c):
    k = np.arange(128)[:, None]
    q = np.arange(128)[None, :]
    ones = np.ones((128, 128), np.float32)
    strict = (k < q).astype(np.float32)
    meta = np.zeros((128, 128), np.float32)
    meta[112:, :] = 1.0
    if c == 0:
        visA, visB = ones, strict
    else:
        visA, visB = strict, np.zeros((128, 128), np.float32)
    vis = np.stack([visA, visB, meta, visB * meta], 1)
    pen = np.where(vis > 0, 0.0, NEG).astype(np.float32)
    pen4 = np.repeat(pen[:, :, None, :], 4, axis=2)
    j = np.arange(128)[:, None]
    kk = np.arange(128)[None, :]
    ntri = np.where(j >= kk, -1.0, 0.0).astype(np.float32)
    return {"sbvis": vis.reshape(128, 512).astype(NPBF), "sbpen": pen4.reshape(128, 4 * 512).astype(NPBF),
            "ntri": ntri.astype(NPBF), "nones": (-np.ones((128, 128), np.float32)).astype(NPBF)}


def phase_sb(kb, io, attn):
    for hf in range(2):
        with ExitStack() as st:
            sb = lambda n, s, d: kb.sb(st, n, s, d)
            KT = sb("KT", [128, 4, NG], BF16)
            VV = sb("VV", [128, NB, 512], BF16)
            ident = sb("ident", [128, 128], BF16)
            vis = sb("vis", [128, 4, 128], BF16)
            pen = sb("pen", [128, 4, 512], BF16)
            ntri = sb("ntri", [128, 128], BF16)
            nones = sb("nones", [128, 128], BF16)
            qraw = Ring([sb(f"qr{i}", [128, 4, 128], BF16) for i in range(2)])
            qzs = [Ring([sb(f"qz{p_}_{i}", [128, 4, 128], BF16) for i in range(2)]) for p_ in range(2)]
            for p_ in range(2):
                for t_ in qzs[p_].items:
                    kb.memset("pool", t_[:], 0.0)
            es = Ring([sb(f"e{i}", [128, 4, 128], F32) for i in range(2)])
            sps = Ring([sb(f"sp{i}", [128, 4, 128], BF16) for i in range(3)])
            spm = Ring([sb(f"spm{i}", [128, 4, 128], BF16) for i in range(2)])
            as_ = Ring([sb(f"a{i}", [128, 4, 128], BF16) for i in range(3)])
            laccs = Ring([sb(f"lacc{i}", [128, 4, 128], BF16) for i in range(2)])
            pzs = Ring([kb.ps(st, f"pz{i}", [128, 4, 128], F32) for i in range(2)])
            prs = Ring([kb.ps(st, f"pr{i}", [128, 4, 128], F32) for i in range(2)])
            pos = Ring([kb.ps(st, f"po{i}", [128, 4, 128], F32) for i in range(2)])
            kb.dma("sp", ident[:], io["ident"][:])
            kb.dma("sp", vis[:], V(io["sbvis"].t.rearrange("p (t q) -> p t q", t=4), []))
            kb.dma("sp", pen[:], V(io["sbpen"].t.rearrange("p (t q) -> p t q", t=4), []))
            kb.dma("sp", ntri[:], io["ntri"][:])
            kb.dma("sp", nones[:], io["nones"][:])
            for ch in range(4):
                kb.dma("sp", KT.p(ch)[:, ch, :], io["KT1g"][(hf * 4 + ch) * 128:(hf * 4 + ch + 1) * 128, :])
            for B in range(NB):
                kb.dma("sp" if B % 2 else "qact", VV.p(B)[:, B, :], io["V1g"][B * 128:(B + 1) * 128, hf * 512:(hf + 1) * 512])
            KTall = lambda ap: V(ap, KT.all().res)
            VVall = lambda ap: V(ap, VV.all().res)
            flat = lambda t_: V(t_.t[:, :, :].rearrange("p h q -> p (h q)"), t_[:].res)
            for i in range(NS):
                cols = slice(i * 128, (i + 1) * 128)
                qr = qraw.next()
                kb.dma("sp", qr[:], V(io["QT1"].t[hf * 512:(hf + 1) * 512, cols].rearrange("(c p) n -> p c n", p=128), []))
                for par in range(2):
                    hp = slice(par * 64, par * 64 + 64)
                    qs = qzs[par].next()
                    kb.ts("pool", qs[hp, :, :], qr[hp, :, :], 0.125, None, ALU.mult)
                    po = pos.next()
                    lacc = None
                    blocks = list(range(2 * i + 1, 0, -1))
                    for bi, B in enumerate(blocks):
                        mt = None
                        if B == 2 * i + 1:
                            mt = 3 if B == 1 else 1
                        elif B == 2 * i:
                            mt = 0
                        elif B == 1:
                            mt = 2
                        pz = pzs.next()
                        for hh in range(4):
                            kb.mm(pz[:, hh, :], KTall(KT.t[:, hh, B * 128:(B + 1) * 128]), qs[:, hh, :])
                        e = es.next()
                        kb.act(e[:], pz[:], AF.Exp)
                        sp = sps.next()
                        kb.act(sp[:], e[:], AF.Ln, bias=1.0)
                        if mt is not None:
                            sm = spm.next()
                            kb.tt("dve", sm[:], sp[:], V(vis.t[:, mt, :].unsqueeze(1).to_broadcast([128, 4, 128]), [vis.r]), ALU.mult)
                            sp = sm
                        pr = prs.next()
                        for hh in range(4):
                            kb.mm(pr[:, hh, :], KTall(KT.t[:, hh, B * 128:(B + 1) * 128]), qs[:, hh, :], start=(hh == 0), stop=False)
                        kb.mm(flat(pr), ntri[:], flat(sp), start=False, stop=(lacc is None and mt is None))
                        if lacc is not None:
                            kb.mm(flat(pr), nones[:], flat(lacc), start=False, stop=(mt is None))
                        if mt is not None:
                            kb.mm(flat(pr), ident[:], pen[:, mt, :], start=False, stop=True)
                        a = as_.next()
                        kb.act(a[:], pr[:], AF.Exp)
                        if bi < len(blocks) - 1:
                            if lacc is None:
                                lacc = laccs.next()
                                kb.copy("pool", lacc[:], sp[:])
                            else:
                                nl = laccs.next()
                                kb.tt("pool", nl[:], lacc[:], sp[:], ALU.add)
                                lacc = nl
                        for hh in range(4):
                            col = hh * 128 + par * 64
                            kb.mm(po[:, hh, 0:64], a[:, hh, :], VVall(VV.t[:, B, col:col + 64]), start=(bi == 0 and hh == 0), stop=(bi == len(blocks) - 1))
                    dst = V(attn.t[:, i, hf * 512:(hf + 1) * 512].rearrange("p (h t d) -> p h t d", h=4, t=2)[:, :, par, :], [attn.r])
                    kb.copy("dve", dst, po[:, :, 0:64])
            kb.P.flush()


def phase_sb_out(kb, io, attn, hin, hout):
    with ExitStack() as st:
        sb = lambda n, s, d: kb.sb(st, n, s, d)
        ident = sb("ident", [128, 128], BF16)
        wo = sb("wo1", [128, 8, 1024], BF16)
        aTs = Ring([sb(f"aT{i}", [128, 8, 128], BF16) for i in range(2)])
        xss = Ring([sb(f"xs{i}", [128, 1024], F32) for i in range(2)])
        hos = Ring([sb(f"ho{i}", [128, 1024], F32) for i in range(2)])
        psT = Ring([kb.ps(st, f"psT{i}", [128, 8, 128], BF16) for i in range(2)])
        psX = Ring([kb.ps(st, f"psX{i}", [128, 512], F32) for i in range(2)])
        kb.dma("sp", ident[:], io["ident"][:])
        for half in range(2):
            src = io["w_out1"].t[:, half * 512:(half + 1) * 512].rearrange("(kc p) m -> p kc m", p=128)
            kb.dma("qpool", wo.p(half)[:, :, half * 512:(half + 1) * 512], V(src, []))
        for i in range(NS):
            rows = slice(i * 128, (i + 1) * 128)
            pT = psT.next()
            for j in range(8):
                kb.tr(pT[:, j, :], attn[:, i, j * 128:(j + 1) * 128], ident[:])
            aT = aTs.next()
            kb.evac(aT[:], pT[:])
            xs = xss.next()
            kb.dma("sp", xs[:], io[hin][rows, :])
            ho = hos.next()
            for half in range(2):
                pw = psX.next()
                for kc in range(8):
                    kb.mm(pw[:], aT[:, kc, :], wo.p(half)[:, kc, half * 512:(half + 1) * 512], start=(kc == 0), stop=(kc == 7))
                kb.tt("dve", ho[:, half * 512:(half + 1) * 512], pw[:], xs[:, half * 512:(half + 1) * 512], ALU.add)
            kb.dma("sp", io[hout][rows, :], ho[:])
        kb.P.flush()


def phase_mixer1(kb, io):
    with ExitStack() as st:
        attn = kb.sb(st, "attn", [128, NS, 1024], BF16)
        phase_sb(kb, io, attn)
        phase_sb_out(kb, io, attn, "H2", "H3")


SPEC_E_IN = [("QT1", [1024, NT], BF16), ("KT1g", [1024, NG], BF16), ("V1g", [NG, 1024], BF16), ("H2", [NT, D], F32),
             ("ident", [128, 128], BF16), ("sbvis", [128, 512], BF16), ("sbpen", [128, 2048], BF16),
             ("ntri", [128, 128], BF16), ("nones", [128, 128], BF16), ("w_out1", [1024, 1024], F32)]
SPEC_E_OUT = [("H3", [NT, D], F32)]


SPEC_F_IN = [("H3", [NT, D], F32), ("HALO3", [NS * 2, D], F32), ("ident", [128, 128], BF16), ("gfinal", [128, D], F32)] + ffn_specs(1)
SPEC_F_OUT = [("H4", [NT, D], F32), ("OUT", [NT, D], F32)]
NCORES = 8


def _run(nc, maps):
    res = run_bass_kernel_spmd(nc, maps, core_ids=list(range(NCORES)))
    return res.results


def kernel_unfused(inp):
    bf = lambda a: np.asarray(a)
    x = np.asarray(inp["x"], np.float32)
    meta = np.asarray(inp["meta_tokens"], np.float32)
    ident = np.eye(128).astype(NPBF)
    cores = [(b, c) for b in range(4) for c in range(2)]
    gs = [gstream(x[b], meta) for b in range(4)]
    xs = [take_own(gs[b], c) for (b, c) in cores]
    rots = [rot_tables(c) for c in range(2)]
    dcs = [dsa_consts(c, np.asarray(inp["rel_bias"], np.float32)) for c in range(2)]
    rcs = [ret_consts(c) for c in range(2)]
    scs = [sb_consts(c) for c in range(2)]
    pw = rep128(2.0 ** -(np.arange(NBIS) + 1.0))
    ncA, _ = build_launch([phase_proj0], SPEC_A_IN, SPEC_A_OUT)
    mapsA = []
    for k, (b, c) in enumerate(cores):
        cq, sq, ck, sk = rots[c]
        mapsA.append({"xs": xs[k], "w_in": inp["even_w_in"][0], "gmix0": rep128(inp["norm_mix"][0]), "ident": ident,
                      "cosq": cq, "sinq": sq, "cosk": ck, "sink": sk})
    oA = _run(ncA, mapsA)
    ncB, _ = build_launch([phase_mixer0], SPEC_B_IN, SPEC_B_OUT)
    mapsB = []
    for k, (b, c) in enumerate(cores):
        o0, o1 = oA[2 * b], oA[2 * b + 1]
        mp = lambda nm, ax: merge_pair(np.asarray(o0[nm]), np.asarray(o1[nm]), axis=ax)
        me = oA[k]
        mapsB.append({"QT": me["QT"], "QIT": me["QIT"], "IW": me["IW"], "KTg": mp("KT", 1), "KITg": mp("KIT", 1), "Vg": mp("V", 0),
                      "ident": ident, "ebraw": dcs[c]["ebraw"], "pen": dcs[c]["pen"], "metaM": dcs[c]["metaM"], "b31": dcs[c]["b31"],
                      "pw": pw, "BQ": me["BQ"], "BKg": mp("BK", 0), "BVg": mp("BV", 0), "BG": me["BG"], "xs": xs[k],
                      "w_out0": inp["even_w_out"][0], "gngain": rep128(inp["even_gn_gain"][0]),
                      "DA": rcs[c]["DA"], "DB": rcs[c]["DB"], "XI": rcs[c]["XI"], "ZT": rcs[c]["ZT"]})
    oB = _run(ncB, mapsB)
    ncC, _ = build_launch(launch_C_phases(), SPEC_C_IN, SPEC_C_OUT)
    f0 = ffn_host_inputs(inp, 0)
    mapsC = []
    for k, (b, c) in enumerate(cores):
        halos = make_halo(np.asarray(oB[2 * b]["H1"]), np.asarray(oB[2 * b + 1]["H1"]))
        m = {"H1": oB[k]["H1"], "HALO1": halos[c], "ident": ident, "gmix1": rep128(inp["norm_mix"][1]), "w_in1": inp["odd_w_in"][0]}
        m.update(f0)
        mapsC.append(m)
    oC = _run(ncC, mapsC)
    ncE, _ = build_launch([phase_mixer1], SPEC_E_IN, SPEC_E_OUT)
    mapsE = []
    for k, (b, c) in enumerate(cores):
        o0, o1 = oC[2 * b], oC[2 * b + 1]
        m = {"QT1": oC[k]["QT1"], "KT1g": merge_pair(np.asarray(o0["KT1"]), np.asarray(o1["KT1"]), axis=1),
             "V1g": merge_pair(np.asarray(o0["V1"]), np.asarray(o1["V1"]), axis=0), "H2": oC[k]["H2"], "ident": ident,
             "w_out1": inp["odd_w_out"][0]}
        m.update(scs[c])
        mapsE.append(m)
    oE = _run(ncE, mapsE)
    ncF, _ = build_launch([lambda kb, io: phase_ffn(kb, io, 1, "H3", "HALO3", "H4", final_out="OUT")], SPEC_F_IN, SPEC_F_OUT)
    f1 = ffn_host_inputs(inp, 1)
    mapsF = []
    for k, (b, c) in enumerate(cores):
        halos = make_halo(np.asarray(oE[2 * b]["H3"]), np.asarray(oE[2 * b + 1]["H3"]))
        m = {"H3": oE[k]["H3"], "HALO3": halos[c], "ident": ident, "gfinal": rep128(inp["norm_final"])}
        m.update(f1)
        mapsF.append(m)
    oF = _run(ncF, mapsF)
    out = np.empty((4, 4096, D), np.float32)
    for b in range(4):
        g = merge_pair(np.asarray(oF[2 * b]["OUT"]), np.asarray(oF[2 * b + 1]["OUT"]), axis=0)
        out[b] = g[256:256 + 4096]
    return out


def kernel(**inputs):
    inp = {k: np.asarray(v) for k, v in inputs.items()}
    return kernel_unfused(inp)
```
